# Optimizing a Trainium2 kernel written in Bass

```python
import jax, jax.numpy as jnp
from jax import lax
import numpy as np

D_MODEL = 2048
BATCH = 1
SEQ = 16384
DEPTH = 4
DEC_BATCH = 8
DEC_SEQ = 16
PAST_LEN = 4096

CHUNK = 64
MIX_WIDTH = D_MODEL
POOL_WIDTH = MIX_WIDTH // 2
POOL_WINDOWS = (2, 4, 8, 16)
POOL_GROUPS = len(POOL_WINDOWS)
POOL_GROUP_WIDTH = POOL_WIDTH // POOL_GROUPS
POOL_HIST = max(POOL_WINDOWS) - 1
GLA_HEADS = 4
GLA_WIDTH = MIX_WIDTH - POOL_WIDTH
GLA_DV = GLA_WIDTH // GLA_HEADS
GLA_DK = GLA_DV // 2
GLA_KEY_WIDTH = GLA_HEADS * GLA_DK
GLA_RANK = 16
GLA_TAU = 16.0
D_FF = 4 * D_MODEL
N_SUB = 3
EPS = 1e-6
IN_WIDTH = POOL_WIDTH + 2 * GLA_KEY_WIDTH + 2 * GLA_WIDTH + GLA_RANK
IN_SPLITS = (POOL_WIDTH,
             POOL_WIDTH + GLA_KEY_WIDTH,
             POOL_WIDTH + 2 * GLA_KEY_WIDTH,
             POOL_WIDTH + 2 * GLA_KEY_WIDTH + GLA_WIDTH,
             POOL_WIDTH + 2 * GLA_KEY_WIDTH + 2 * GLA_WIDTH)

kernel_name = "hybrid_pool_gla_stream_step"


def _rmsnorm(x, g):
    xf = x.astype(jnp.float32)
    y = xf * lax.rsqrt(jnp.mean(xf * xf, axis=-1, keepdims=True) + EPS) * g.astype(jnp.float32)
    return y.astype(x.dtype)


def _swiglu(h, wg, wu, wd):
    return (jax.nn.silu(h @ wg) * (h @ wu)) @ wd


def _pool_mixer(u, prev, start, w_pool, pool_scale):
    L = u.shape[1]
    full = jnp.concatenate([prev.astype(u.dtype), u], axis=1)
    cs = jnp.cumsum(full.astype(jnp.float32), axis=1)
    cs = jnp.concatenate([jnp.zeros_like(cs[:, :1]), cs], axis=1)
    pos = start + jnp.arange(L)
    end = cs[:, POOL_HIST + 1:POOL_HIST + 1 + L]
    outs = []
    for gi, w in enumerate(POOL_WINDOWS):
        sl = slice(gi * POOL_GROUP_WIDTH, (gi + 1) * POOL_GROUP_WIDTH)
        wsum = end[..., sl] - cs[:, POOL_HIST + 1 - w:POOL_HIST + 1 - w + L, sl]
        cnt = jnp.minimum(pos + 1, w).astype(jnp.float32)[None, :, None]
        outs.append(wsum / cnt - u[..., sl].astype(jnp.float32))
    m = jnp.stack(outs, axis=2).astype(u.dtype)
    y = jnp.einsum('blgc,gcd->blgd', m, w_pool).reshape(u.shape) * pool_scale
    return y, full[:, -POOL_HIST:]


def _gla(q, k, v, logf, s0):
    Bn, L = q.shape[0], q.shape[1]
    pad = (-L) % CHUNK
    f32 = jnp.float32
    def blk(a):
        a = a.astype(f32)
        if pad:
            a = jnp.pad(a, ((0, 0), (0, pad), (0, 0), (0, 0)))
        n = a.shape[1] // CHUNK
        return a.reshape(Bn, n, CHUNK, GLA_HEADS, a.shape[-1]).transpose(0, 3, 1, 2, 4)
    q, k, v, logf = blk(q), blk(k), blk(v), blk(logf)
    n_blk = q.shape[2]
    b = jnp.cumsum(logf, axis=3)
    b_last = b[:, :, :, -1:, :]
    qe = q * jnp.exp(b) * (GLA_DK ** -0.5)
    ke = k * jnp.exp(-b)
    kd = k * jnp.exp(b_last - b)
    mask = jnp.tril(jnp.ones((CHUNK, CHUNK), dtype=bool))
    att = jnp.where(mask, jnp.einsum('bhnik,bhnjk->bhnij', qe, ke), 0.0)
    o = jnp.einsum('bhnij,bhnjv->bhniv', att, v)
    kv = jnp.einsum('bhnjk,bhnjv->bhnkv', kd, v)
    decay = jnp.exp(b_last[:, :, :, 0, :])
    def step(S, inp):
        d, u = inp
        return d[..., None] * S + u, S
    s_fin, s_start = lax.scan(step, s0.astype(f32), (jnp.moveaxis(decay, 2, 0), jnp.moveaxis(kv, 2, 0)))
    s_start = jnp.moveaxis(s_start, 0, 2)
    o = o + jnp.einsum('bhnik,bhnkv->bhniv', qe, s_start)
    o = o.transpose(0, 2, 3, 1, 4).reshape(Bn, n_blk * CHUNK, GLA_HEADS, GLA_DV)[:, :L]
    return o, s_fin


def _layer(x, c, pool_prev, gla_prev, start, w_ada, b_ada, norm_pre, norm_post, w_ffn_gate, w_ffn_up,
           w_ffn_down, w_in, w_forget, b_forget, w_pool, pool_scale, gla_norm, w_out):
    Bn, L, _ = x.shape
    mod = jax.nn.silu(c.astype(jnp.float32)) @ w_ada.astype(jnp.float32) + b_ada.astype(jnp.float32)
    mod = mod.astype(x.dtype).reshape(Bn, N_SUB, 3, 1, D_MODEL)

    def pre(h, i):
        return _rmsnorm(h, norm_pre[i]) * (1 + mod[:, i, 1]) + mod[:, i, 0]

    def post(h, i):
        return mod[:, i, 2] * _rmsnorm(h, norm_post[i])

    h = pre(x, 0)
    x = x + 0.5 * post(_swiglu(h, w_ffn_gate[0], w_ffn_up[0], w_ffn_down[0]), 0)

    h = pre(x, 1)
    z = h @ w_in
    u, q, k, v, og, fr = jnp.split(z, IN_SPLITS, axis=-1)
    y_pool, pool_new = _pool_mixer(u, pool_prev, start, w_pool, pool_scale)
    logf = jax.nn.log_sigmoid((fr @ w_forget + b_forget).astype(jnp.float32)) / GLA_TAU
    o, gla_new = _gla(q.reshape(Bn, L, GLA_HEADS, GLA_DK), k.reshape(Bn, L, GLA_HEADS, GLA_DK),
                      v.reshape(Bn, L, GLA_HEADS, GLA_DV), logf.reshape(Bn, L, GLA_HEADS, GLA_DK), gla_prev)
    o = _rmsnorm(o.astype(x.dtype), gla_norm.reshape(GLA_HEADS, GLA_DV)).reshape(Bn, L, GLA_WIDTH)
    o = o * jax.nn.silu(og)
    mix = jnp.concatenate([y_pool, o], axis=-1) @ w_out
    x = x + post(mix, 1)

    h = pre(x, 2)
    x = x + 0.5 * post(_swiglu(h, w_ffn_gate[1], w_ffn_up[1], w_ffn_down[1]), 2)
    return x, pool_new, gla_new


def setup_inputs(seed: int = 0) -> dict:
    key = jax.random.key(seed)
    ks = jax.random.split(key, 20)
    def nrm(k, shape, s):
        return jax.random.normal(k, shape, jnp.float32) * s
    return {
        "x_prompt": nrm(ks[0], (BATCH, SEQ, D_MODEL), 1.0),
        "x_sample": nrm(ks[1], (DEC_BATCH, DEC_SEQ, D_MODEL), 1.0),
        "state_pool": nrm(ks[2], (DEPTH, DEC_BATCH, POOL_HIST, POOL_WIDTH), 1.0),
        "state_gla": nrm(ks[3], (DEPTH, DEC_BATCH, GLA_HEADS, GLA_DK, GLA_DV), 1.0),
        "c_prompt": nrm(ks[4], (BATCH, D_MODEL), 1.0),
        "c_sample": nrm(ks[5], (DEC_BATCH, D_MODEL), 1.0),
        "w_ada": nrm(ks[6], (DEPTH, D_MODEL, N_SUB * 3 * D_MODEL), 0.5 * D_MODEL ** -0.5),
        "b_ada": nrm(ks[7], (DEPTH, N_SUB * 3 * D_MODEL), 0.02),
        "norm_pre": 1.0 + nrm(ks[8], (DEPTH, N_SUB, D_MODEL), 0.02),
        "norm_post": 1.0 + nrm(ks[9], (DEPTH, N_SUB, D_MODEL), 0.02),
        "w_ffn_gate": nrm(ks[10], (DEPTH, 2, D_MODEL, D_FF), D_MODEL ** -0.5),
        "w_ffn_up": nrm(ks[11], (DEPTH, 2, D_MODEL, D_FF), D_MODEL ** -0.5),
        "w_ffn_down": nrm(ks[12], (DEPTH, 2, D_FF, D_MODEL), D_FF ** -0.5),
        "w_in": nrm(ks[13], (DEPTH, D_MODEL, IN_WIDTH), D_MODEL ** -0.5),
        "w_forget": nrm(ks[14], (DEPTH, GLA_RANK, GLA_KEY_WIDTH), GLA_RANK ** -0.5),
        "b_forget": nrm(ks[15], (DEPTH, GLA_KEY_WIDTH), 0.1),
        "w_pool": nrm(ks[16], (DEPTH, POOL_GROUPS, POOL_GROUP_WIDTH, POOL_GROUP_WIDTH), POOL_GROUP_WIDTH ** -0.5),
        "pool_scale": 1.0 + nrm(ks[17], (DEPTH, POOL_WIDTH), 0.1),
        "gla_norm": 1.0 + nrm(ks[18], (DEPTH, GLA_WIDTH), 0.02),
        "w_out": nrm(ks[19], (DEPTH, MIX_WIDTH, D_MODEL), MIX_WIDTH ** -0.5),
    }


def reference(x_prompt, x_sample, state_pool, state_gla, c_prompt, c_sample, w_ada, b_ada, norm_pre,
              norm_post, w_ffn_gate, w_ffn_up, w_ffn_down, w_in, w_forget, b_forget, w_pool, pool_scale,
              gla_norm, w_out):
    xp, xs = x_prompt, x_sample
    bp = x_prompt.shape[0]
    pool0 = jnp.zeros((bp, POOL_HIST, POOL_WIDTH), x_prompt.dtype)
    gla0 = jnp.zeros((bp, GLA_HEADS, GLA_DK, GLA_DV), jnp.float32)
    pool_p, gla_p, pool_s, gla_s = [], [], [], []
    for l in range(DEPTH):
        lw = (w_ada[l], b_ada[l], norm_pre[l], norm_post[l], w_ffn_gate[l], w_ffn_up[l], w_ffn_down[l],
              w_in[l], w_forget[l], b_forget[l], w_pool[l], pool_scale[l], gla_norm[l], w_out[l])
        xp, pp, gp = _layer(xp, c_prompt, pool0, gla0, 0, *lw)
        xs, ps, gs = _layer(xs, c_sample, state_pool[l], state_gla[l], PAST_LEN, *lw)
        pool_p.append(pp)
        gla_p.append(gp)
        pool_s.append(ps)
        gla_s.append(gs)
    return (xp, xs, jnp.stack(pool_p), jnp.stack(gla_p), jnp.stack(pool_s), jnp.stack(gla_s))
```

```python
import os
import numpy as np
import concourse.bass as bass
import concourse.mybir as mybir
from concourse.bass_utils import run_bass_kernel_spmd

F32 = mybir.dt.float32
BF16 = mybir.dt.bfloat16
AF = mybir.ActivationFunctionType
ALU = mybir.AluOpType

NCORE = 8
L = 4
D = 2048
KC = 16
FC = 64
TP = 2048
TS = 16
T = TP + TS
TT = 528
NT = 4
MW = 512
NMT = 4
EPS = 1e-6
NPAY = 4 + 1024 + 120
SLOT = 8192
NSLOT = 4
PSUM_BASE = 1 << 24
DRAM_BASE = 1 << 26

CB_ONESD, CB_ONESV, CB_TRILE, CB_TRIGT, CB_NEG = 0, 128, 256, 384, 512
NCB = 516
CF_MASK4, CF_RANK, CF_SEL, CF_INV = 0, 512, 520, 528
NCF = 528 + 64


ALLOC_LOG = []


class Buf:
    def __init__(self, arena, off, shape, dt, parts=128):
        self.off, self.shape, self.dt, self.parts = off, tuple(shape), dt, parts
        self.esz = 4 if dt == F32 else 2
        n = 1
        for s in shape:
            n *= s
        self.nbytes = n * self.esz
        ap = arena[0:parts, off // 2:(off + self.nbytes) // 2]
        if dt == F32:
            ap = ap.bitcast(F32)
        if len(shape) == 2:
            ap = ap.rearrange("p (a b) -> p a b", a=shape[0])
        elif len(shape) == 3:
            ap = ap.rearrange("p (a b c) -> p a b c", a=shape[0], b=shape[1])
        elif len(shape) == 4:
            ap = ap.rearrange("p (a b c d) -> p a b c d", a=shape[0], b=shape[1], c=shape[2])
        self.ap = ap

    def r(self, *idx):
        stride = self.nbytes
        lo, hi = self.off, self.off + self.nbytes
        for k, i in enumerate(idx):
            stride //= self.shape[k]
            if isinstance(i, tuple):
                a, b = i
                lo, hi = lo + a * stride, lo + b * stride
                break
            lo, hi = lo + i * stride, lo + (i + 1) * stride
        return (lo, hi)


class Kern:
    BK = 1024

    def __init__(self):
        self.ops = {e: [] for e in ("pe", "act", "dve", "pool", "sp")}
        self.clk = {}
        self.seen = {e: {} for e in self.ops}
        self.wrec = {}
        self.rrec = {}
        self.named = {}
        self.dsems = []
        self.know = {e: {} for e in self.ops}
        self.vc = {}

    def _buckets(self, s, e):
        return range(s // self.BK, (e - 1) // self.BK + 1)

    def _deps(self, reads, writes, own=None):
        deps = {}

        def add(k, v):
            if deps.get(k, 0) < v:
                deps[k] = v
        for (s, e) in reads:
            for b in self._buckets(s, e):
                for (rs, re_, k, v) in self.wrec.get(b, ()):
                    if rs < e and s < re_:
                        add(k, v)
        for (s, e) in writes:
            for b in self._buckets(s, e):
                for (rs, re_, k, v) in self.wrec.get(b, ()):
                    if rs < e and s < re_ and k != own:
                        add(k, v)
                for (rs, re_, k), v in self.rrec.get(b, {}).items():
                    if rs < e and s < re_ and k != own:
                        add(k, v)
        return deps

    def _record(self, reads, writes, key, val):
        for (s, e) in writes:
            for b in self._buckets(s, e):
                lst = [r for r in self.wrec.get(b, ()) if not (s <= r[0] and r[1] <= e)]
                lst.append((s, e, key, val))
                self.wrec[b] = lst
                d = self.rrec.get(b)
                if d:
                    for kk in [kk for kk in d if s <= kk[0] and kk[1] <= e]:
                        del d[kk]
        for (s, e) in reads:
            for b in self._buckets(s, e):
                self.rrec.setdefault(b, {})[(s, e, key)] = val

    def xr(self, c0, c1):
        return (DRAM_BASE + (1 << 20) + c0, DRAM_BASE + (1 << 20) + c1)

    def dr(self, name, idx=0):
        k = (name, idx)
        if k not in self.named:
            self.named[k] = DRAM_BASE + len(self.named) * self.BK
        s = self.named[k]
        return (s, s + 1)

    def _waits(self, eng, deps, skip_own=False):
        w = []
        know = self.know[eng]
        for k, v in sorted(deps.items(), key=lambda kv: -kv[1]):
            if skip_own and k == eng:
                continue
            if know.get(k, 0) >= v:
                continue
            w.append((k, v))
            know[k] = v
            for kk, vv in self.vc.get((k, v), {}).items():
                if know.get(kk, 0) < vv:
                    know[kk] = vv
        return w

    def _stamp(self, eng, key, val):
        d = dict(self.know[eng])
        d[key] = val
        self.vc[(key, val)] = d

    def op(self, eng, fn, reads=(), writes=()):
        deps = self._deps(reads, writes)
        waits = self._waits(eng, deps, skip_own=(eng == "pe"))
        self.clk[eng] = self.clk.get(eng, 0) + 1
        self._stamp(eng, eng, self.clk[eng])
        self.ops[eng].append((waits, fn, (eng, 1)))
        self._record(reads, writes, eng, self.clk[eng])

    def pe(self, fn, reads=(), writes=()):
        self.op("pe", fn, reads, writes)

    def act(self, fn, reads=(), writes=()):
        self.op("act", fn, reads, writes)

    def dve(self, fn, reads=(), writes=()):
        self.op("dve", fn, reads, writes)

    def dma(self, queue, dsem, out, in_, reads=(), writes=(), implied=False):
        if dsem not in self.dsems:
            self.dsems.append(dsem)
        deps = self._deps(reads, writes)
        prev = self.clk.get(dsem, 0)
        if implied:
            deps.pop(dsem, None)
        elif prev:
            deps[dsem] = max(deps.get(dsem, 0), prev)
        waits = self._waits(queue, deps)
        self.clk[dsem] = prev + 16
        self._stamp(queue, dsem, self.clk[dsem])
        self.ops[queue].append((waits, lambda e, o=out, i=in_: e.dma_start(out=o, in_=i), (dsem, 16)))
        self._record(reads, writes, dsem, self.clk[dsem])

    def special(self, queue, key, fn, reads=(), writes=()):
        if key not in self.dsems:
            self.dsems.append(key)
        deps = self._deps(reads, writes)
        waits = self._waits(queue, deps)
        self.clk[key] = self.clk.get(key, 0) + 1
        self._stamp(queue, key, self.clk[key])
        self.ops[queue].append((waits, fn, (key, None)))
        self._record(reads, writes, key, self.clk[key])

    def final_waits(self, eng):
        waits = []
        for k, v in self.clk.items():
            if self.know[eng].get(k, 0) < v:
                self.know[eng][k] = v
                waits.append((k, v))
        self.ops[eng].append((waits, None, None))


def build(depth=L, ntile_dbg=None):
    nc = bass.Bass("TRN2", target_bir_lowering=False)
    K = Kern()

    def din(name, shape):
        return nc.dram_tensor(name, list(shape), F32, kind="ExternalInput").ap()

    def dout(name, shape):
        return nc.dram_tensor(name, list(shape), F32, kind="ExternalOutput").ap()

    xT = din("xT", [128, KC * T]).rearrange("p (c t) -> p c t", c=KC)
    cbc_in = din("cbc", [128, 4096])
    wada = din("wada", [depth * 72 * 128, 4096])
    bada = din("bada", [128, depth * 144])
    npre = din("npre", [128, depth * 48])
    npost = din("npost", [128, depth * 48])
    wgu = din("wgu", [depth * 2 * 64 * 128, 4096])
    wd = din("wd", [depth * 2 * 16 * 2 * 128, 4096])
    winf = din("winf", [depth * 16 * 128, 4096])
    winfr = din("winfr", [depth * 128, 256])
    wint = din("wint", [depth * 6 * 128, 4096])
    wfa = din("wfa", [depth * 17, 512])
    wpool = din("wpool", [depth * 128, 2048])
    pscale = din("pscale", [128, depth * 8])
    gnorm = din("gnorm", [128, depth * 8])
    wout = din("wout", [depth * 8 * 128, 4096])
    spool = din("spool", [depth * 128, 120])
    sgla = din("sgla", [depth * 128, 1024])
    cbf = din("cbf", [128, NCB])
    cf32 = din("cf32", [128, NCF])

    yT = dout("yT", [128, KC * T]).rearrange("p (c t) -> p c t", c=KC)
    npp = dout("npp", [depth * 128, 120])
    ngp = dout("ngp", [depth * 128, 1024])
    nps = dout("nps", [depth * 128, 120])
    ngs = dout("ngs", [depth * 128, 1024])

    xs = nc.dram_tensor("xs", [128, KC * T], F32, kind="Internal").ap().rearrange("p (c t) -> p c t", c=KC)
    cin = nc.dram_tensor("cin", [depth * 128, NPAY], F32, kind="Internal").ap()
    cout = nc.dram_tensor("cout", [depth * NCORE * 128, NPAY], F32, kind="Internal").ap()

    ARENA_BYTES = 204 * 1024
    import contextlib
    es = contextlib.ExitStack()
    arena_t = es.enter_context(nc.sbuf_tensor("arena", [128, ARENA_BYTES // 2], BF16))
    psum_t = es.enter_context(nc.psum_tensor("ps", [128, 8, 512], F32))
    arena = arena_t[:]
    psum = psum_t[:]

    ptr = [0]

    def alloc(shape, dt, parts=128):
        b = Buf(arena, ptr[0], shape, dt, parts)
        ALLOC_LOG.append((ptr[0], tuple(shape), str(dt)))
        ptr[0] += (b.nbytes + 31) // 32 * 32
        return b

    CBF = alloc([NCB], BF16)
    CF = alloc([NCF], F32)
    MOD = alloc([depth, 9, KC, 2], F32)
    GS = alloc([depth, 3, KC, 2], F32)
    CG = alloc([depth, 3, KC, 2], F32)
    BADA = alloc([depth, 144], F32)
    NPRE = alloc([depth, 3, KC], F32)
    NPOST = alloc([depth, 3, KC], F32)
    PSC = alloc([depth, 8], F32)
    GNO = alloc([depth, 8], F32)
    CTB = alloc([KC, 2], F32)
    SCB = alloc([KC, 2], BF16)
    PAY = alloc([NPAY], F32)
    SST = alloc([4, 256], F32)
    SBF = [alloc([4, 256], BF16), alloc([4, 256], BF16)]
    DEC = alloc([4], F32)
    DM = alloc([4], F32)
    HALOB = alloc([8, 15], F32)
    WFR = alloc([KC, 16], BF16)
    WFA = alloc([512], BF16, parts=32)
    WPL = alloc([4, 2, 256], BF16)
    RING = [alloc([SLOT // 2], BF16) for _ in range(NSLOT)]
    EPSB = alloc([1], F32)
    ONEB = alloc([1], F32)
    WORK0 = ptr[0]

    def ps(b0, b1=None, n=512):
        if b1 is None:
            return psum[:, b0, 0:n]
        return psum[:, b0:b1, 0:n]

    def pr(b0, b1=None):
        b1 = b0 + 1 if b1 is None else b1
        return (PSUM_BASE + b0 * 2048, PSUM_BASE + b1 * 2048)

    ones_d = CBF.ap[:, CB_ONESD:CB_ONESD + 128]
    ones_v = CBF.ap[:, CB_ONESV:CB_ONESV + 128]
    triLE = CBF.ap[:, CB_TRILE:CB_TRILE + 128]
    triGT = CBF.ap[:, CB_TRIGT:CB_TRIGT + 128]
    negcol = CBF.ap[:, CB_NEG:CB_NEG + 1]
    mask4 = CF.ap[:, CF_MASK4:CF_MASK4 + 512].rearrange("p (h t) -> p h t", h=4)
    inv16 = CF.ap[:, CF_INV:CF_INV + 64].rearrange("p (g t) -> p g t", g=4)

    ring_ctr = [0]

    def wload(src, shape):
        s = ring_ctr[0] % NSLOT
        ring_ctr[0] += 1
        rb = RING[s]
        n = 1
        for d_ in shape:
            n *= d_
        view = Buf(arena, rb.off, shape, BF16)
        assert n * 2 <= SLOT, (n, shape)
        K.dma("pool", "w%d" % s, arena[:, rb.off // 2: rb.off // 2 + n], src, reads=[], writes=[(rb.off, rb.off + n * 2)], implied=True)
        return view

    def sload(buf, src, rd=()):
        K.dma("sp", "ld", buf.ap if len(buf.shape) == 1 else arena_view(buf), src, reads=list(rd), writes=[buf.r()])

    def arena_view(buf):
        ap = arena[0:buf.parts, buf.off // 2:(buf.off + buf.nbytes) // 2]
        return ap.bitcast(F32) if buf.dt == F32 else ap

    K.dma("pool", "cb", arena_view(CBF), cbf, writes=[CBF.r()])
    sload(CF, cf32)
    sload(BADA, bada)
    sload(NPRE, npre)
    sload(NPOST, npost)
    sload(PSC, pscale)
    sload(GNO, gnorm)

    ptr[0] = WORK0
    CBC = alloc([2, 2048], F32)
    SC2 = alloc([2, 2048], BF16)
    JUNK = alloc([2048], BF16)
    MRAW = alloc([144, 2], F32)
    K.dma("sp", "ld", arena_view(CBC), cbc_in, writes=[CBC.r()])
    K.act(lambda e: e.activation(out=arena_view(SC2), in_=arena_view(CBC), func=AF.Silu), reads=[CBC.r()], writes=[SC2.r()])
    for l in range(depth):
        K.dve(lambda e: e.memset(arena_view(MRAW), 0.0), writes=[MRAW.r()])
        for u in range(72):
            wv = wload(wada[(l * 72 + u) * 128:(l * 72 + u + 1) * 128, :], [2, 2048])
            for o2 in range(2):
                oc = u * 2 + o2
                for s in range(2):
                    K.dve(lambda e, wv=wv, o2=o2, oc=oc, s=s: e.scalar_tensor_tensor(out=JUNK.ap, in0=wv.ap[:, o2, :], scalar=1.0, in1=SC2.ap[:, s, :],
                                                                                   op0=ALU.mult, op1=ALU.mult, accum_out=MRAW.ap[:, oc, s:s + 1]),
                          reads=[wv.r(), SC2.r()], writes=[JUNK.r(), (MRAW.off + (oc * 2 + s) * 4, MRAW.off + (oc * 2 + s) * 4 + 4)])
        modl = MOD.ap[:, l].rearrange("p k c s -> p (k c) s")
        for s in range(2):
            K.dve(lambda e, s=s, modl=modl, l=l: e.tensor_tensor(out=modl[:, :, s], in0=MRAW.ap[:, :, s], in1=BADA.ap[:, l, :], op=ALU.add),
                  reads=[MRAW.r(), BADA.r(l)], writes=[MOD.r(l)])
        for i in range(3):
            fac = 1.0 if i == 1 else 0.5
            for s in range(2):
                K.dve(lambda e, l=l, i=i, s=s: e.scalar_tensor_tensor(out=GS.ap[:, l, i, :, s], in0=MOD.ap[:, l, i * 3 + 1, :, s], scalar=1.0,
                                                                       in1=NPRE.ap[:, l, i, :], op0=ALU.add, op1=ALU.mult),
                      reads=[MOD.r(l), NPRE.r(l)], writes=[GS.r(l, i)])
                K.dve(lambda e, l=l, i=i, s=s, fac=fac: e.scalar_tensor_tensor(out=CG.ap[:, l, i, :, s], in0=MOD.ap[:, l, i * 3 + 2, :, s], scalar=fac,
                                                                                in1=NPOST.ap[:, l, i, :], op0=ALU.mult, op1=ALU.mult),
                      reads=[MOD.r(l), NPOST.r(l)], writes=[CG.r(l, i)])

    def gs_ap(l, i, c, s):
        return GS.ap[:, l, i, c, s:s + 1]

    def sh_ap(l, i, c, s):
        return MOD.ap[:, l, i * 3, c, s:s + 1]

    def cg_ap(l, i, c, s):
        return CG.ap[:, l, i, c, s:s + 1]

    def prenorm(l, i, X, H, RS, TMP, W, segs, halves, sbank):
        K.act(lambda e: e.activation(out=H.ap[:, :, 0:W], in_=X.ap[:, :, 0:W], func=AF.Square), reads=[X.r()], writes=[H.r()])
        rstd_from_sq(H, RS, W, halves, sbank, ones_d)
        for c in range(KC):
            tb = TMP[c % 2]
            K.dve(lambda e, c=c, tb=tb: e.tensor_tensor(out=tb.ap[:, 0:W], in0=X.ap[:, c, 0:W], in1=RS.ap[:, 0:W], op=ALU.mult),
                  reads=[X.r(c), RS.r()], writes=[tb.r()])
            for (a, b, s) in segs:
                K.act(lambda e, c=c, tb=tb, a=a, b=b, s=s: e.activation(out=H.ap[:, c, a:b], in_=tb.ap[:, a:b], func=AF.Identity,
                                                                        scale=gs_ap(l, i, c, s), bias=sh_ap(l, i, c, s)),
                      reads=[tb.r(), GS.r(l, i), MOD.r(l)], writes=[H.r(c)])

    def rstd_from_sq(SQ, RS, W, halves, sbank, ones):
        def f(e):
            last = None
            for hi, (a, b) in enumerate(halves):
                for c in range(KC):
                    last = e.matmul(psum[:, sbank + hi, 0:b - a], lhsT=ones, rhs=SQ.ap[:, c, a:b], start=(c == 0), stop=(c == KC - 1))
            return last
        K.pe(f, reads=[SQ.r(), CBF.r()], writes=[pr(sbank, sbank + len(halves))])
        for hi, (a, b) in enumerate(halves):
            K.act(lambda e, hi=hi, a=a, b=b: e.activation(out=RS.ap[:, a:b], in_=psum[:, sbank + hi, 0:b - a], func=AF.Sqrt, bias=EPSB.ap[:, 0:1]),
                  reads=[pr(sbank + hi), EPSB.r()], writes=[RS.r()])
        K.dve(lambda e: e.reciprocal(out=RS.ap[:, 0:W], in_=RS.ap[:, 0:W]), reads=[RS.r()], writes=[RS.r()])

    def postnorm_residual(l, i, YB, SQ, RS, TMP, X2, W, segs, halves, sbank, xsrc, xdst, xkey):
        rstd_from_sq(SQ, RS, W, halves, sbank, ones_d)
        K.dma("sp", "xl", X2.ap[:, :, 0:W], xsrc, reads=[xkey], writes=[X2.r()])
        for m in range(KC):
            tb = TMP[m % 2]
            K.dve(lambda e, m=m, tb=tb: e.tensor_tensor(out=tb.ap[:, 0:W], in0=YB.ap[:, m, 0:W], in1=RS.ap[:, 0:W], op=ALU.mult),
                  reads=[YB.r(m), RS.r()], writes=[tb.r()])
            for (a, b, s) in segs:
                K.dve(lambda e, m=m, tb=tb, a=a, b=b, s=s: e.scalar_tensor_tensor(out=X2.ap[:, m, a:b], in0=tb.ap[:, a:b], scalar=cg_ap(l, i, m, s),
                                                                                  in1=X2.ap[:, m, a:b], op0=ALU.mult, op1=ALU.add),
                      reads=[tb.r(), CG.r(l, i), X2.r(m)], writes=[X2.r(m)])
        K.dma("sp", "xst", xdst, X2.ap[:, :, 0:W], reads=[X2.r()], writes=[xkey])

    K.dve(lambda e: e.memset(EPSB.ap[:, 0:1], EPS), writes=[EPSB.r()])

    def ffn(l, f, i, src, dst):
        ptr[0] = WORK0
        XY = alloc([KC, TT], F32)
        H = alloc([KC, TT], BF16)
        A = alloc([32, TT], BF16)
        SG = [alloc([TT], F32), alloc([TT], F32)]
        RS = alloc([TT], F32)
        TMP = [alloc([TT], F32), alloc([TT], F32)]
        X2 = Buf(arena, A.off, [KC, TT], F32)
        assert X2.nbytes <= A.nbytes and ptr[0] <= ARENA_BYTES, ptr[0]
        ntile = NT if ntile_dbg is None else ntile_dbg
        for t in range(ntile):
            c0 = t * 512
            last_t = (t == NT - 1)
            W = 528 if last_t else 512
            c1 = c0 + W
            segs = [(0, 512, 0), (512, 528, 1)] if last_t else [(0, 512, 0)]
            pieces = [(0, 512), (512, 528)] if last_t else [(0, 512)]
            xkey = K.xr(c0, c1)
            K.dma("sp", "xl", XY.ap[:, :, 0:W], src[:, :, c0:c1], reads=[xkey], writes=[XY.r()])
            prenorm(l, i, XY, H, RS, TMP, W, segs, pieces, 6)
            for hh in range(2):
                for jj in range(32):
                    j = hh * 32 + jj
                    r0 = ((l * 2 + f) * 64 + j) * 128
                    wv = wload(wgu[r0:r0 + 128, :], [2, KC, 128])
                    pb = (j % 2) * 3

                    def fm(e, wv=wv, pb=pb, last_t=last_t):
                        last = None
                        for gu in range(2):
                            for kc in range(KC):
                                last = e.matmul(psum[:, pb + gu, 0:512], lhsT=wv.ap[:, gu, kc, :], rhs=H.ap[:, kc, 0:512],
                                                start=(kc == 0), stop=(kc == KC - 1))
                                if last_t:
                                    last = e.matmul(psum[:, pb + 2, gu * 16:gu * 16 + 16], lhsT=wv.ap[:, gu, kc, :], rhs=H.ap[:, kc, 512:528],
                                                    start=(kc == 0), stop=(kc == KC - 1))
                        return last
                    K.pe(fm, reads=[wv.r(), H.r()], writes=[pr(pb, pb + 3)])
                    sg = SG[j % 2]
                    K.act(lambda e, sg=sg, pb=pb: e.activation(out=sg.ap[:, 0:512], in_=psum[:, pb, 0:512], func=AF.Silu),
                          reads=[pr(pb)], writes=[sg.r()])
                    K.dve(lambda e, sg=sg, pb=pb, jj=jj: e.tensor_tensor(out=A.ap[:, jj, 0:512], in0=sg.ap[:, 0:512], in1=psum[:, pb + 1, 0:512], op=ALU.mult),
                          reads=[sg.r(), pr(pb + 1)], writes=[A.r(jj)])
                    if last_t:
                        K.act(lambda e, sg=sg, pb=pb: e.activation(out=sg.ap[:, 512:528], in_=psum[:, pb + 2, 0:16], func=AF.Silu),
                              reads=[pr(pb + 2)], writes=[sg.r()])
                        K.dve(lambda e, sg=sg, pb=pb, jj=jj: e.tensor_tensor(out=A.ap[:, jj, 512:528], in0=sg.ap[:, 512:528], in1=psum[:, pb + 2, 16:32], op=ALU.mult),
                              reads=[sg.r(), pr(pb + 2)], writes=[A.r(jj)])
                for m in range(KC):
                    r0 = (((l * 2 + f) * 16 + m) * 2 + hh) * 128
                    wv = wload(wd[r0:r0 + 128, :], [32, 128])
                    yb = (m % 2) * 2

                    def fm2(e, wv=wv, yb=yb, last_t=last_t):
                        last = None
                        for jj in range(32):
                            last = e.matmul(psum[:, yb, 0:512], lhsT=wv.ap[:, jj, :], rhs=A.ap[:, jj, 0:512], start=(jj == 0), stop=(jj == 31))
                            if last_t:
                                last = e.matmul(psum[:, yb + 1, 0:16], lhsT=wv.ap[:, jj, :], rhs=A.ap[:, jj, 512:528], start=(jj == 0), stop=(jj == 31))
                        return last
                    K.pe(fm2, reads=[wv.r(), A.r()], writes=[pr(yb, yb + 2)])
                    evs = [(0, 512, yb, 0)] + ([(512, 528, yb + 1, 0)] if last_t else [])
                    for (a, b, bank, pc) in evs:
                        if hh == 0:
                            K.act(lambda e, m=m, a=a, b=b, bank=bank, pc=pc: e.activation(out=XY.ap[:, m, a:b], in_=psum[:, bank, pc:pc + b - a], func=AF.Copy),
                                  reads=[pr(bank)], writes=[XY.r(m)])
                        else:
                            K.dve(lambda e, m=m, a=a, b=b, bank=bank, pc=pc: e.tensor_tensor(out=XY.ap[:, m, a:b], in0=psum[:, bank, pc:pc + b - a], in1=XY.ap[:, m, a:b], op=ALU.add),
                                  reads=[pr(bank), XY.r(m)], writes=[XY.r(m)])
                    if hh == 1:
                        K.act(lambda e, m=m, W=W: e.activation(out=H.ap[:, m, 0:W], in_=XY.ap[:, m, 0:W], func=AF.Square),
                              reads=[XY.r(m)], writes=[H.r(m)])
            postnorm_residual(l, i, XY, H, RS, TMP, X2, W, segs, pieces, 6, src[:, :, c0:c1], dst[:, :, c0:c1], xkey)

    def mixer_bufs():
        ptr[0] = WORK0
        B = {}
        B["XT"] = alloc([KC, MW], F32)
        B["H"] = alloc([KC, MW], BF16)
        B["UT"] = alloc([8, 528], F32)
        B["YB"] = Buf(arena, B["H"].off, [KC, MW], F32)
        assert B["YB"].nbytes <= B["H"].nbytes + B["UT"].nbytes
        B["SA"] = Buf(arena, B["XT"].off, [8, 528], F32)
        B["SB"] = Buf(arena, B["XT"].off + B["SA"].nbytes, [6, 528], F32)
        assert B["SA"].nbytes + B["SB"].nbytes <= B["XT"].nbytes
        o = B["XT"].off
        B["ATTM"] = Buf(arena, o, [4, 128], BF16); o += 1024
        B["OSB"] = Buf(arena, o, [8, 128], F32); o += 4096
        B["OSQ"] = Buf(arena, o, [8, 128], BF16); o += 2048
        B["RH"] = Buf(arena, o, [4, 128], F32); o += 2048
        B["T1"] = Buf(arena, o, [8, 128], F32); o += 4096
        B["MIX"] = alloc([KC, MW], BF16)
        B["EB"] = Buf(arena, B["MIX"].off, [4, MW], F32)
        B["ENB"] = Buf(arena, B["MIX"].off + 8192, [4, MW], F32)
        B["QE"] = alloc([4, MW], BF16)
        B["KE"] = alloc([4, MW], BF16)
        B["SOG"] = alloc([8, MW], BF16)
        B["SQ2"] = Buf(arena, B["QE"].off, [KC, MW], BF16)
        B["EXPD"] = alloc([4, 512], F32)
        B["M"] = Buf(arena, B["EXPD"].off, [8, MW], BF16)
        B["GT"] = Buf(arena, B["MIX"].off, [NCORE, NPAY], F32)
        assert B["GT"].off + B["GT"].nbytes <= B["EXPD"].off + B["EXPD"].nbytes
        B["VT"] = alloc([4, 1024], BF16)
        B["KD"] = alloc([4, 512], BF16)
        B["LL"] = alloc([4, 512], BF16)
        B["FRT"] = alloc([MW], BF16, parts=32)
        B["ETMP"] = alloc([512], F32)
        B["RS"] = alloc([MW], F32)
        B["TMP"] = [alloc([MW], F32), alloc([MW], F32)]
        assert ptr[0] <= ARENA_BYTES, ptr[0]
        return B

    gbank = [0]

    def nb():
        b = gbank[0] % 4
        gbank[0] += 1
        return b

    def mixer_front(l, B, W, blk, nblk, xsrc, xkey, seq, full):
        XT, H, RS, TMP = B["XT"], B["H"], B["RS"], B["TMP"]
        K.dma("sp", "xl", XT.ap[:, :, 0:W], xsrc, reads=[xkey], writes=[XT.r()])
        prenorm(l, 1, XT, H, RS, TMP, W, [(0, W, seq)], [(0, W)], 7)
        FRT, LL, EXPD, ETMP = B["FRT"], B["LL"], B["EXPD"], B["ETMP"]
        K.dve(lambda e: e.memset(FRT.ap[0:32, 0:W], 1.0), writes=[FRT.r()])
        bk = nb()

        def ffr(e):
            last = None
            for kc in range(KC):
                last = e.matmul(psum[0:16, bk, 0:W], lhsT=WFR.ap[:, kc, :], rhs=H.ap[:, kc, 0:W], start=(kc == 0), stop=(kc == KC - 1))
            return last
        K.pe(ffr, reads=[WFR.r(), H.r()], writes=[pr(bk)])
        K.act(lambda e: e.activation(out=FRT.ap[0:16, 0:W], in_=psum[0:16, bk, 0:W], func=AF.Copy), reads=[pr(bk)], writes=[FRT.r()])
        for tb in range(nblk):
            t0 = tb * blk
            b1 = nb()
            K.pe(lambda e, t0=t0, b1=b1: e.matmul(psum[0:blk, b1, 0:512], lhsT=FRT.ap[0:17, t0:t0 + blk], rhs=WFA.ap[0:17, :], start=True, stop=True),
                 reads=[FRT.r(), WFA.r()], writes=[pr(b1)])
            K.act(lambda e, b1=b1: e.activation(out=ETMP.ap[0:blk, :], in_=psum[0:blk, b1, 0:512], func=AF.Exp, scale=-1.0),
                  reads=[pr(b1)], writes=[ETMP.r()])
            K.act(lambda e, tb=tb: e.activation(out=LL.ap[0:blk, tb, :], in_=ETMP.ap[0:blk, :], func=AF.Ln, bias=ONEB.ap[0:blk, 0:1]),
                  reads=[ETMP.r(), ONEB.r()], writes=[LL.r(tb)])
            b2 = nb()
            K.pe(lambda e, tb=tb, b2=b2: e.matmul(psum[0:blk, b2, 0:512], lhsT=triGT[0:blk, 0:blk], rhs=LL.ap[0:blk, tb, :], start=True, stop=True),
                 reads=[LL.r(tb), CBF.r()], writes=[pr(b2)])
            K.act(lambda e, tb=tb, b2=b2: e.activation(out=EXPD.ap[0:blk, tb, :], in_=psum[0:blk, b2, 0:512], func=AF.Exp),
                  reads=[pr(b2)], writes=[EXPD.r(tb)])
            if full:
                EB, ENB = B["EB"], B["ENB"]

                def fbt(e, tb=tb):
                    last = None
                    for h in range(4):
                        last = e.matmul(psum[:, 7, h * 128:h * 128 + blk], lhsT=LL.ap[0:blk, tb, h * 128:(h + 1) * 128], rhs=triLE[0:blk, 0:blk],
                                        start=True, stop=True)
                    return last
                K.pe(fbt, reads=[LL.r(tb), CBF.r()], writes=[pr(7)])
                bT = psum[:, 7, :].rearrange("p (h t) -> p h t", h=4)[:, :, 0:blk]
                K.act(lambda e, t0=t0, bT=bT: e.activation(out=EB.ap[:, :, t0:t0 + blk], in_=bT, func=AF.Exp), reads=[pr(7)], writes=[EB.r()])
                K.act(lambda e, t0=t0, bT=bT: e.activation(out=ENB.ap[:, :, t0:t0 + blk], in_=bT, func=AF.Exp, scale=-1.0), reads=[pr(7)], writes=[ENB.r()])
        KD, VT = B["KD"], B["VT"]
        for cb in range(3):
            wva = wload(wint[(l * 6 + cb * 2) * 128:(l * 6 + cb * 2 + 1) * 128, :], [8, 512])
            wvb = wload(wint[(l * 6 + cb * 2 + 1) * 128:(l * 6 + cb * 2 + 2) * 128, :], [8, 512])
            for tb in range(nblk):
                t0 = tb * blk
                b1 = nb()

                def ftm(e, wva=wva, wvb=wvb, t0=t0, b1=b1):
                    last = None
                    for kc in range(KC):
                        w_ = wva.ap[:, kc, :] if kc < 8 else wvb.ap[:, kc - 8, :]
                        last = e.matmul(psum[0:blk, b1, 0:512], lhsT=H.ap[:, kc, t0:t0 + blk], rhs=w_, start=(kc == 0), stop=(kc == KC - 1))
                    return last
                K.pe(ftm, reads=[wva.r(), wvb.r(), H.r()], writes=[pr(b1)])
                if cb == 0:
                    K.dve(lambda e, tb=tb, b1=b1: e.tensor_tensor(out=KD.ap[0:blk, tb, :], in0=psum[0:blk, b1, 0:512], in1=EXPD.ap[0:blk, tb, :], op=ALU.mult),
                          reads=[pr(b1), EXPD.r(tb)], writes=[KD.r(tb)])
                else:
                    K.act(lambda e, tb=tb, b1=b1, cb=cb: e.activation(out=VT.ap[0:blk, tb, (cb - 1) * 512:cb * 512], in_=psum[0:blk, b1, 0:512], func=AF.Copy),
                          reads=[pr(b1)], writes=[VT.r(tb)])

    def kv_and_decay(B, S, tb, blk, dacc=None):
        KD, VT, LL = B["KD"], B["VT"], B["LL"]
        kb = (nb() // 2) * 2
        gbank[0] = kb + 2

        def fkv(e):
            last = None
            for h in range(4):
                last = e.matmul(psum[:, kb + h // 2, (h % 2) * 256:(h % 2) * 256 + 256], lhsT=KD.ap[0:blk, tb, h * 128:(h + 1) * 128],
                                rhs=VT.ap[0:blk, tb, h * 256:(h + 1) * 256], start=True, stop=True)
            return last
        K.pe(fkv, reads=[KD.r(tb), VT.r(tb)], writes=[pr(kb, kb + 2)])

        def fbl(e):
            last = None
            for h in range(4):
                last = e.matmul(psum[:, 7, 2 * h:2 * h + 1], lhsT=LL.ap[0:blk, tb, h * 128:(h + 1) * 128], rhs=negcol[0:blk, 0:1], start=True, stop=True)
            return last
        K.pe(fbl, reads=[LL.r(tb), CBF.r()], writes=[pr(7)])
        K.act(lambda e: e.activation(out=DEC.ap[:, 0:4], in_=psum[:, 7, 0:8].rearrange("p (h two) -> p h two", two=2)[:, :, 0], func=AF.Exp),
              reads=[pr(7)], writes=[DEC.r()])
        kvp = psum[:, kb:kb + 2, :].rearrange("p b (h v) -> p (b h) v", h=2)
        for h in range(4):
            K.dve(lambda e, h=h, kvp=kvp: e.scalar_tensor_tensor(out=S[:, h, :], in0=S[:, h, :], scalar=DEC.ap[:, h:h + 1], in1=kvp[:, h, :],
                                                                 op0=ALU.mult, op1=ALU.add),
                  reads=[S_r, DEC.r(), pr(kb, kb + 2)], writes=[S_r])
        if dacc is not None:
            K.dve(lambda e: e.tensor_tensor(out=dacc, in0=dacc, in1=DEC.ap[:, 0:4], op=ALU.mult), reads=[PAY.r(), DEC.r()], writes=[PAY.r()])

    S_r = None

    K.dve(lambda e: e.memset(ONEB.ap[:, 0:1], 1.0), writes=[ONEB.r()])

    def mixer(l, src, dst):
        nonlocal S_r
        B = mixer_bufs()
        K.dma("pool", "wsm", arena_view(WFR), winfr[l * 128:(l + 1) * 128, :], writes=[WFR.r()])
        K.dma("pool", "wsm", arena[0:17, WFA.off // 2:WFA.off // 2 + 512], wfa[l * 17:(l + 1) * 17, :], writes=[WFA.r()])
        K.dma("pool", "wsm", arena_view(WPL), wpool[l * 128:(l + 1) * 128, :], writes=[WPL.r()])
        pay = PAY.ap
        SL = pay[:, 4:1028].rearrange("p (h v) -> p h v", h=4)
        S_r = PAY.r()
        K.dve(lambda e: e.memset(pay[:, 0:4], 1.0), writes=[PAY.r()])
        K.dve(lambda e: e.memset(pay[:, 4:NPAY], 0.0), writes=[PAY.r()])
        for t in range(NMT):
            c0 = t * MW
            mixer_front(l, B, MW, 128, 4, src[:, :, c0:c0 + MW], K.xr(c0, c0 + MW), 0, full=False)
            for tb in range(4):
                kv_and_decay(B, SL, tb, 128, dacc=pay[:, 0:4])
            if t == NMT - 1:
                H = B["H"]
                hb = nb()
                for uu in range(4):
                    wv = wload(winf[(l * 16 + uu) * 128:(l * 16 + uu + 1) * 128, :], [2, KC, 128])

                    def fh(e, wv=wv, uu=uu):
                        last = None
                        for o4 in range(2):
                            oc = uu * 2 + o4
                            for kc in range(KC):
                                last = e.matmul(psum[:, hb, oc * 16:oc * 16 + 15], lhsT=wv.ap[:, o4, kc, :], rhs=H.ap[:, kc, MW - 15:MW],
                                                start=(kc == 0), stop=(kc == KC - 1))
                        return last
                    K.pe(fh, reads=[wv.r(), H.r()], writes=[pr(hb)])
                K.act(lambda e: e.activation(out=pay[:, 1028:NPAY].rearrange("p (o t) -> p o t", o=8),
                                             in_=psum[:, hb, 0:128].rearrange("p (o t) -> p o t", o=8)[:, :, 0:15], func=AF.Copy),
                      reads=[pr(hb)], writes=[PAY.r()])
        K.dma("sp", "cio", cin[l * 128:(l + 1) * 128, :], pay, reads=[PAY.r()], writes=[K.dr("cin", l)])
        K.special("pool", "cc%d" % l,
                  lambda e, l=l: e.collective_compute("AllGather", ALU.bypass, replica_groups=[list(range(NCORE))],
                                                       ins=[cin[l * 128:(l + 1) * 128, :]], outs=[cout[l * NCORE * 128:(l + 1) * NCORE * 128, :]]),
                  reads=[K.dr("cin", l)], writes=[K.dr("cout", l)])
        GT = B["GT"]
        K.dma("sp", "cio", GT.ap, cout[l * NCORE * 128:(l + 1) * NCORE * 128, :].rearrange("(r p) c -> p r c", p=128),
              reads=[K.dr("cout", l)], writes=[GT.r()])
        S = SST.ap
        S_r = SST.r()
        rank = CF.ap[:, CF_RANK:CF_RANK + 8]
        sel = CF.ap[:, CF_SEL:CF_SEL + 8]
        K.dve(lambda e: e.memset(S, 0.0), writes=[SST.r()])
        UM = B["XT"]
        umap = arena[:, UM.off // 2:UM.off // 2 + 2048].bitcast(F32).rearrange("p (h v) -> p h v", h=4)
        for j in range(NCORE):
            K.dve(lambda e, j=j: e.tensor_scalar(out=DM.ap[:, 0:4], in0=GT.ap[:, j, 0:4], scalar1=-1.0, scalar2=rank[:, j:j + 1], op0=ALU.add, op1=ALU.mult),
                  reads=[GT.r(j), CF.r()], writes=[DM.r()])
            K.dve(lambda e: e.tensor_single_scalar(out=DM.ap[:, 0:4], in_=DM.ap[:, 0:4], scalar=1.0, op=ALU.add), reads=[DM.r()], writes=[DM.r()])
            K.dve(lambda e, j=j: e.tensor_scalar(out=umap, in0=GT.ap[:, j, 4:1028].rearrange("p (h v) -> p h v", h=4), scalar1=rank[:, j:j + 1], scalar2=None, op0=ALU.mult),
                  reads=[GT.r(j), CF.r()], writes=[UM.r()])
            for h in range(4):
                K.dve(lambda e, h=h: e.scalar_tensor_tensor(out=S[:, h, :], in0=S[:, h, :], scalar=DM.ap[:, h:h + 1], in1=umap[:, h, :], op0=ALU.mult, op1=ALU.add),
                      reads=[SST.r(), DM.r(), UM.r()], writes=[SST.r()])
        hb_ap = arena_view(HALOB)
        K.dve(lambda e: e.tensor_scalar(out=hb_ap, in0=GT.ap[:, 0, 1028:NPAY], scalar1=sel[:, 0:1], scalar2=None, op0=ALU.mult),
              reads=[GT.r(0), CF.r()], writes=[HALOB.r()])
        for j in range(1, NCORE):
            K.dve(lambda e, j=j: e.scalar_tensor_tensor(out=hb_ap, in0=GT.ap[:, j, 1028:NPAY], scalar=sel[:, j:j + 1], in1=hb_ap, op0=ALU.mult, op1=ALU.add),
                  reads=[GT.r(j), CF.r(), HALOB.r()], writes=[HALOB.r()])
        K.act(lambda e: e.activation(out=SBF[0].ap, in_=S, func=AF.Copy), reads=[SST.r()], writes=[SBF[0].r()])
        sbf_i = [0]

        def pass_b(W, blk, nblk, c0, seq, first):
            XT, H, UT, M, MIX = B["XT"], B["H"], B["UT"], B["M"], B["MIX"]
            QE, KE, SOG, VT, KD = B["QE"], B["KE"], B["SOG"], B["VT"], B["KD"]
            EB, ENB = B["EB"], B["ENB"]
            xkey = K.xr(c0, c0 + W)
            mixer_front(l, B, W, blk, nblk, src[:, :, c0:c0 + W], xkey, seq, full=True)
            for unit in (6, 7, 4, 5, 0, 1, 2, 3, 12, 13, 14, 15):
                wv = wload(winf[(l * 16 + unit) * 128:(l * 16 + unit + 1) * 128, :], [2, KC, 128])
                for o4 in range(2):
                    oc_ = unit * 2 + o4
                    b1 = nb()

                    def fp(e, wv=wv, o4=o4, b1=b1):
                        last = None
                        for kc in range(KC):
                            last = e.matmul(psum[:, b1, 0:W], lhsT=wv.ap[:, o4, kc, :], rhs=H.ap[:, kc, 0:W], start=(kc == 0), stop=(kc == KC - 1))
                        return last
                    K.pe(fp, reads=[wv.r(), H.r()], writes=[pr(b1)])
                    if unit in (6, 7):
                        o4 = oc_ - 12
                        K.dve(lambda e, o4=o4, b1=b1: e.tensor_tensor(out=KE.ap[:, o4, 0:W], in0=psum[:, b1, 0:W], in1=ENB.ap[:, o4, 0:W], op=ALU.mult),
                              reads=[pr(b1), ENB.r()], writes=[KE.r(o4)])
                    elif unit in (4, 5):
                        o4 = oc_ - 8
                        K.dve(lambda e, o4=o4, b1=b1: e.scalar_tensor_tensor(out=QE.ap[:, o4, 0:W], in0=psum[:, b1, 0:W], scalar=128.0 ** -0.5, in1=EB.ap[:, o4, 0:W],
                                                                              op0=ALU.mult, op1=ALU.mult),
                              reads=[pr(b1), EB.r()], writes=[QE.r(o4)])
                    elif unit < 4:
                        oc = oc_
                        K.act(lambda e, oc=oc, b1=b1: e.activation(out=UT.ap[:, oc, 15:15 + W], in_=psum[:, b1, 0:W], func=AF.Copy),
                              reads=[pr(b1)], writes=[UT.r(oc)])
                    else:
                        vc = oc_ - 24
                        K.act(lambda e, vc=vc, b1=b1: e.activation(out=SOG.ap[:, vc, 0:W], in_=psum[:, b1, 0:W], func=AF.Silu),
                              reads=[pr(b1)], writes=[SOG.r(vc)])
            SA, SB_ = B["SA"], B["SB"]
            NW = 15 + W
            K.dve(lambda e: e.tensor_copy(out=UT.ap[:, :, 0:15], in_=HALOB.ap), reads=[HALOB.r()], writes=[UT.r()])
            K.dve(lambda e: e.tensor_tensor(out=SA.ap[:, :, 1:NW], in0=UT.ap[:, :, 1:NW], in1=UT.ap[:, :, 0:NW - 1], op=ALU.add),
                  reads=[UT.r(), XT.r()], writes=[SA.r()])
            K.dve(lambda e: e.tensor_tensor(out=SB_.ap[:, :, 3:NW], in0=SA.ap[:, 2:8, 3:NW], in1=SA.ap[:, 2:8, 1:NW - 2], op=ALU.add),
                  reads=[SA.r()], writes=[SB_.r()])
            K.dve(lambda e: e.tensor_tensor(out=SA.ap[:, 4:8, 7:NW], in0=SB_.ap[:, 2:6, 7:NW], in1=SB_.ap[:, 2:6, 3:NW - 4], op=ALU.add),
                  reads=[SB_.r()], writes=[SA.r()])
            K.dve(lambda e: e.tensor_tensor(out=SB_.ap[:, 4:6, 15:NW], in0=SA.ap[:, 6:8, 15:NW], in1=SA.ap[:, 6:8, 7:NW - 8], op=ALU.add),
                  reads=[SA.r()], writes=[SB_.r()])
            wsrc = [(SA, 0), (SB_, 0), (SA, 4), (SB_, 4)]
            for g in range(4):
                wb, wo = wsrc[g]
                K.dve(lambda e, g=g, wb=wb, wo=wo: e.scalar_tensor_tensor(out=M.ap[:, 2 * g:2 * g + 2, 0:W], in0=wb.ap[:, wo:wo + 2, 15:NW], scalar=1.0 / (2 << g),
                                                                           in1=UT.ap[:, 2 * g:2 * g + 2, 15:NW], op0=ALU.mult, op1=ALU.subtract),
                      reads=[SA.r(), SB_.r(), UT.r()], writes=[M.r()])
            if first:
                T16 = B["TMP"][0]
                for g in range(4):
                    wb, wo = wsrc[g]
                    for cc in range(2):
                        K.dve(lambda e, g=g, wb=wb, wo=wo, cc=cc: e.tensor_tensor(out=T16.ap[:, 0:16], in0=wb.ap[:, wo + cc, 15:31], in1=inv16[:, g, :], op=ALU.mult),
                              reads=[SA.r(), SB_.r(), CF.r()], writes=[T16.r()])
                        K.dve(lambda e, g=g, cc=cc: e.tensor_tensor(out=M.ap[:, 2 * g + cc, 0:16], in0=T16.ap[:, 0:16], in1=UT.ap[:, 2 * g + cc, 15:31], op=ALU.subtract),
                              reads=[T16.r(), UT.r()], writes=[M.r()])
            K.dve(lambda e: e.tensor_copy(out=HALOB.ap, in_=UT.ap[:, :, W:W + 15]), reads=[UT.r()], writes=[HALOB.r()])
            for g in range(4):
                for dc in range(2):
                    oc = g * 2 + dc
                    b1 = nb()

                    def fpl(e, g=g, dc=dc, b1=b1):
                        last = None
                        for cc in range(2):
                            last = e.matmul(psum[:, b1, 0:W], lhsT=WPL.ap[:, g, cc, dc * 128:(dc + 1) * 128], rhs=M.ap[:, g * 2 + cc, 0:W], start=(cc == 0), stop=(cc == 1))
                        return last
                    K.pe(fpl, reads=[WPL.r(), M.r()], writes=[pr(b1)])
                    K.act(lambda e, oc=oc, b1=b1: e.activation(out=MIX.ap[:, oc, 0:W], in_=psum[:, b1, 0:W], func=AF.Identity, scale=PSC.ap[:, l, oc:oc + 1]),
                          reads=[pr(b1), PSC.r()], writes=[MIX.r(oc)])
            ATTM, OSB, OSQ, RH, T1 = B["ATTM"], B["OSB"], B["OSQ"], B["RH"], B["T1"]
            for tb in range(nblk):
                t0 = tb * blk

                def fatt(e, t0=t0):
                    last = None
                    for h in range(4):
                        last = e.matmul(psum[0:blk, 4, h * 128:h * 128 + blk], lhsT=KE.ap[:, h, t0:t0 + blk], rhs=QE.ap[:, h, t0:t0 + blk], start=True, stop=True)
                    return last
                K.pe(fatt, reads=[KE.r(), QE.r()], writes=[pr(4)])
                attp = psum[0:blk, 4, :].rearrange("p (h t) -> p h t", h=4)[:, :, 0:blk]
                K.dve(lambda e, attp=attp: e.tensor_tensor(out=ATTM.ap[0:blk, :, 0:blk], in0=attp, in1=mask4[0:blk, :, 0:blk], op=ALU.mult),
                      reads=[pr(4), CF.r()], writes=[ATTM.r()])
                sb = SBF[sbf_i[0] % 2]

                def fo(e, tb=tb, t0=t0, sb=sb):
                    last = None
                    for h in range(4):
                        for v2 in range(2):
                            vc = h * 2 + v2
                            o_ap = psum[:, 5 + vc // 4, (vc % 4) * 128:(vc % 4) * 128 + blk]
                            e.matmul(o_ap, lhsT=VT.ap[0:blk, tb, vc * 128:(vc + 1) * 128], rhs=ATTM.ap[0:blk, h, 0:blk], start=True, stop=False)
                            last = e.matmul(o_ap, lhsT=sb.ap[:, h, v2 * 128:(v2 + 1) * 128], rhs=QE.ap[:, h, t0:t0 + blk], start=False, stop=True)
                    return last
                K.pe(fo, reads=[VT.r(tb), ATTM.r(), sb.r(), QE.r()], writes=[pr(5, 7)])
                kv_and_decay(B, S, tb, blk)
                sbf_i[0] += 1
                sbn = SBF[sbf_i[0] % 2]
                K.act(lambda e, sbn=sbn: e.activation(out=sbn.ap, in_=S, func=AF.Copy), reads=[SST.r()], writes=[sbn.r()])
                op_ = psum[:, 5:7, :].rearrange("p b (v t) -> p (b v) t", v=4)[:, :, 0:blk]
                K.act(lambda e, op_=op_: e.activation(out=OSB.ap[:, :, 0:blk], in_=op_, func=AF.Copy), reads=[pr(5, 7)], writes=[OSB.r()])
                K.act(lambda e, op_=op_: e.activation(out=OSQ.ap[:, :, 0:blk], in_=op_, func=AF.Square), reads=[pr(5, 7)], writes=[OSQ.r()])

                def frh(e):
                    last = None
                    for h in range(4):
                        for v2 in range(2):
                            last = e.matmul(psum[:, 7, h * 128:h * 128 + blk], lhsT=ones_v, rhs=OSQ.ap[:, h * 2 + v2, 0:blk], start=(v2 == 0), stop=(v2 == 1))
                    return last
                K.pe(frh, reads=[OSQ.r(), CBF.r()], writes=[pr(7)])
                rhp = psum[:, 7, :].rearrange("p (h t) -> p h t", h=4)[:, :, 0:blk]
                K.act(lambda e, rhp=rhp: e.activation(out=RH.ap[:, :, 0:blk], in_=rhp, func=AF.Sqrt, bias=EPSB.ap[:, 0:1]), reads=[pr(7), EPSB.r()], writes=[RH.r()])
                K.dve(lambda e: e.reciprocal(out=RH.ap[:, :, 0:blk], in_=RH.ap[:, :, 0:blk]), reads=[RH.r()], writes=[RH.r()])
                for v2 in range(2):
                    K.dve(lambda e, v2=v2: e.tensor_tensor(out=T1.ap.rearrange("p (h two) t -> p h two t", two=2)[:, :, v2, 0:blk],
                                                           in0=OSB.ap.rearrange("p (h two) t -> p h two t", two=2)[:, :, v2, 0:blk],
                                                           in1=RH.ap[:, :, 0:blk], op=ALU.mult),
                          reads=[OSB.r(), RH.r()], writes=[T1.r()])
                for vc in range(8):
                    K.dve(lambda e, vc=vc, t0=t0: e.scalar_tensor_tensor(out=MIX.ap[:, 8 + vc, t0:t0 + blk], in0=T1.ap[:, vc, 0:blk], scalar=GNO.ap[:, l, vc:vc + 1],
                                                                          in1=SOG.ap[:, vc, t0:t0 + blk], op0=ALU.mult, op1=ALU.mult),
                          reads=[T1.r(), GNO.r(), SOG.r(vc)], writes=[MIX.r(8 + vc)])
            YB, SQ2 = B["YB"], B["SQ2"]
            for unit in range(8):
                wv = wload(wout[(l * 8 + unit) * 128:(l * 8 + unit + 1) * 128, :], [2, KC, 128])
                for m4 in range(2):
                    m = unit * 2 + m4
                    b1 = nb()

                    def fw(e, wv=wv, m4=m4, b1=b1):
                        last = None
                        for kc in range(KC):
                            last = e.matmul(psum[:, b1, 0:W], lhsT=wv.ap[:, m4, kc, :], rhs=MIX.ap[:, kc, 0:W], start=(kc == 0), stop=(kc == KC - 1))
                        return last
                    K.pe(fw, reads=[wv.r(), MIX.r()], writes=[pr(b1)])
                    K.act(lambda e, m=m, b1=b1: e.activation(out=YB.ap[:, m, 0:W], in_=psum[:, b1, 0:W], func=AF.Copy), reads=[pr(b1)], writes=[YB.r(m)])
                    K.act(lambda e, m=m, b1=b1: e.activation(out=SQ2.ap[:, m, 0:W], in_=psum[:, b1, 0:W], func=AF.Square), reads=[pr(b1)], writes=[SQ2.r(m)])
            postnorm_residual(l, 1, YB, SQ2, B["RS"], B["TMP"], XT, W, [(0, W, seq)], [(0, W)], 7, src[:, :, c0:c0 + W], dst[:, :, c0:c0 + W], xkey)

        for t in range(NMT):
            pass_b(MW, 128, 4, t * MW, 0, first=(t == 0))
        K.dma("sp", "so", ngp[l * 128:(l + 1) * 128, :], arena_view(SST), reads=[SST.r()], writes=[K.dr("ngp", l)])
        K.dma("sp", "so", npp[l * 128:(l + 1) * 128, :], arena_view(HALOB), reads=[HALOB.r()], writes=[K.dr("npp", l)])
        K.dma("sp", "si", arena_view(SST), sgla[l * 128:(l + 1) * 128, :], reads=[K.dr("ngp", l)], writes=[SST.r()])
        K.dma("sp", "si", arena_view(HALOB), spool[l * 128:(l + 1) * 128, :], reads=[K.dr("npp", l)], writes=[HALOB.r()])
        sbf_i[0] += 1
        sb_s = SBF[sbf_i[0] % 2]
        K.act(lambda e, sb_s=sb_s: e.activation(out=sb_s.ap, in_=S, func=AF.Copy), reads=[SST.r()], writes=[sb_s.r()])
        pass_b(TS, TS, 1, TP, 1, first=False)
        K.dma("sp", "so", ngs[l * 128:(l + 1) * 128, :], arena_view(SST), reads=[SST.r()], writes=[K.dr("ngs", l)])
        K.dma("sp", "so", nps[l * 128:(l + 1) * 128, :], arena_view(HALOB), reads=[HALOB.r()], writes=[K.dr("nps", l)])

    mode = os.environ.get("MK_MODE", "full")
    for l in range(depth):
        src = xT if l == 0 else xs
        if mode in ("full", "ffn"):
            ffn(l, 0, 0, src, xs)
            src = xs
        if mode in ("full", "mix"):
            mixer(l, src, xs)
            src = xs
        if mode in ("full", "ffn"):
            ffn(l, 1, 2, src, yT if l == depth - 1 else xs)
        elif l == depth - 1:
            ptr[0] = WORK0
            CP = alloc([KC, 1032], F32)
            for hh_ in range(2):
                K.dma("sp", "xl", CP.ap, xs[:, :, hh_ * 1032:(hh_ + 1) * 1032], reads=[K.xr(hh_ * 1032, (hh_ + 1) * 1032)], writes=[CP.r()])
                K.dma("sp", "xst", yT[:, :, hh_ * 1032:(hh_ + 1) * 1032], CP.ap, reads=[CP.r()], writes=[K.dr("y", hh_)])
    K.final_waits("sp")

    sems = {}
    for name in ["pe", "act", "dve", "pool"] + K.dsems:
        sems[name] = es.enter_context(nc.semaphore("s_" + name))
    block = es.enter_context(nc.Block())

    def emit(eng_name):
        def body(e):
            for (waits, fn, inc) in K.ops[eng_name]:
                for (k, v) in waits:
                    e.wait_ge(sems[k], v)
                if fn is None:
                    continue
                inst = fn(e)
                if inc[1] is None:
                    inst.then_inc(sems[inc[0]])
                else:
                    inst.then_inc(sems[inc[0]], inc[1])
        return body
    def emit_sp(e):
        emit("sp")(e)
        pass
    block.tensor(emit("pe"))
    block.scalar(emit("act"))
    block.vector(emit("dve"))
    block.gpsimd(emit("pool"))
    block.sync(emit_sp)
    es.close()
    return nc


def _prep_shared(inp, depth, l0=0):
    f = np.float32
    out = {}
    wa = np.asarray(inp["w_ada"][l0:l0 + depth], f)
    out["wada"] = np.ascontiguousarray(wa.reshape(depth, 2048, 72, 2, 128).transpose(0, 2, 4, 3, 1)).reshape(depth * 72 * 128, 4096)
    del wa
    out["bada"] = np.ascontiguousarray(np.asarray(inp["b_ada"][l0:l0 + depth], f).reshape(depth, 144, 128).transpose(2, 0, 1)).reshape(128, depth * 144)
    out["npre"] = np.ascontiguousarray(np.asarray(inp["norm_pre"][l0:l0 + depth], f).reshape(depth, 3, KC, 128).transpose(3, 0, 1, 2)).reshape(128, depth * 48)
    out["npost"] = np.ascontiguousarray(np.asarray(inp["norm_post"][l0:l0 + depth], f).reshape(depth, 3, KC, 128).transpose(3, 0, 1, 2)).reshape(128, depth * 48)
    wg = np.asarray(inp["w_ffn_gate"][l0:l0 + depth], f).reshape(depth, 2, KC, 128, FC, 128)
    wu = np.asarray(inp["w_ffn_up"][l0:l0 + depth], f).reshape(depth, 2, KC, 128, FC, 128)
    wgu = np.empty((depth, 2, FC, 128, 2, KC, 128), f)
    wgu[:, :, :, :, 0] = wg.transpose(0, 1, 4, 3, 2, 5)
    wgu[:, :, :, :, 1] = wu.transpose(0, 1, 4, 3, 2, 5)
    out["wgu"] = wgu.reshape(depth * 2 * 64 * 128, 4096)
    del wg, wu, wgu
    wdn = np.asarray(inp["w_ffn_down"][l0:l0 + depth], f).reshape(depth, 2, 2, 32, 128, KC, 128)
    out["wd"] = np.ascontiguousarray(wdn.transpose(0, 1, 5, 2, 4, 3, 6)).reshape(depth * 2 * 16 * 2 * 128, 4096)
    del wdn
    wi = np.asarray(inp["w_in"][l0:l0 + depth], f)
    wif = wi[:, :, :4096].reshape(depth, KC, 128, 16, 2, 128)
    out["winf"] = np.ascontiguousarray(wif.transpose(0, 3, 2, 4, 1, 5)).reshape(depth * 16 * 128, 4096)
    out["winfr"] = np.ascontiguousarray(wi[:, :, 4096:4112].reshape(depth, KC, 128, 16).transpose(0, 2, 1, 3)).reshape(depth * 128, 256)
    wit = wi[:, :, 1536:3072].reshape(depth, 2, 8, 128, 3, 512)
    out["wint"] = np.ascontiguousarray(wit.transpose(0, 4, 1, 3, 2, 5)).reshape(depth * 6 * 128, 4096)
    del wi
    out["wfa"] = np.ascontiguousarray(np.concatenate([np.asarray(inp["w_forget"][l0:l0 + depth], f), np.asarray(inp["b_forget"][l0:l0 + depth], f)[:, None, :]], axis=1)).reshape(depth * 17, 512)
    wp = np.asarray(inp["w_pool"][l0:l0 + depth], f).reshape(depth, 4, 2, 128, 256)
    out["wpool"] = np.ascontiguousarray(wp.transpose(0, 3, 1, 2, 4)).reshape(depth * 128, 2048)
    out["pscale"] = np.ascontiguousarray(np.asarray(inp["pool_scale"][l0:l0 + depth], f).reshape(depth, 8, 128).transpose(2, 0, 1)).reshape(128, depth * 8)
    out["gnorm"] = np.ascontiguousarray(np.asarray(inp["gla_norm"][l0:l0 + depth], f).reshape(depth, 8, 128).transpose(2, 0, 1)).reshape(128, depth * 8)
    wo = np.asarray(inp["w_out"][l0:l0 + depth], f).reshape(depth, KC, 128, 8, 2, 128)
    out["wout"] = np.ascontiguousarray(wo.transpose(0, 3, 2, 4, 1, 5)).reshape(depth * 8 * 128, 4096)
    cb = np.zeros((128, NCB), f)
    cb[:, CB_ONESD:CB_ONESD + 128] = 1.0 / D
    cb[:, CB_ONESV:CB_ONESV + 128] = 1.0 / 256
    s_ = np.arange(128)[:, None]
    t_ = np.arange(128)[None, :]
    cb[:, CB_TRILE:CB_TRILE + 128] = np.where(s_ <= t_, -1.0 / 16, 0.0)
    cb[:, CB_TRIGT:CB_TRIGT + 128] = np.where(s_ > t_, -1.0 / 16, 0.0)
    cb[:, CB_NEG] = -1.0 / 16
    out["cbf"] = cb
    return out


def _core_inputs(inp, shared, core, depth, l0=0, xT=None):
    f = np.float32
    d = dict(shared)
    xp = np.asarray(inp["x_prompt"], f)[0, core * TP:(core + 1) * TP]
    xsm = np.asarray(inp["x_sample"], f)[core]
    xall = np.concatenate([xp, xsm], axis=0)
    d["xT"] = np.ascontiguousarray(xall.reshape(T, KC, 128).transpose(2, 1, 0)).reshape(128, KC * T) if xT is None else xT
    c2 = np.stack([np.asarray(inp["c_prompt"], f)[0], np.asarray(inp["c_sample"], f)[core]], axis=1)
    d["cbc"] = np.ascontiguousarray(np.broadcast_to(c2.T.reshape(1, 4096), (128, 4096)))
    sp = np.asarray(inp["state_pool"], f)[:depth, core]
    d["spool"] = np.ascontiguousarray(sp.reshape(depth, 15, 8, 128).transpose(0, 3, 2, 1)).reshape(depth * 128, 120)
    sg = np.asarray(inp["state_gla"], f)[:depth, core]
    d["sgla"] = np.ascontiguousarray(sg.transpose(0, 2, 1, 3)).reshape(depth * 128, 1024)
    cf = np.zeros((128, NCF), f)
    j_ = np.arange(128)[:, None]
    i_ = np.arange(128)[None, :]
    m = (j_ <= i_).astype(f)
    cf[:, CF_MASK4:CF_MASK4 + 512] = np.tile(m, (1, 4))
    cf[:, CF_RANK:CF_RANK + 8] = (np.arange(8) < core).astype(f)[None, :]
    cf[:, CF_SEL:CF_SEL + 8] = (np.arange(8) == core - 1).astype(f)[None, :]
    pos = core * TP + np.arange(16)
    for g, w in enumerate((2, 4, 8, 16)):
        cf[:, CF_INV + g * 16:CF_INV + (g + 1) * 16] = (1.0 / np.minimum(pos + 1, w)).astype(f)[None, :]
    d["cf32"] = cf
    return d


_NC_CACHE = {}


def kernel(**inputs):
    depth = int(os.environ.get("MK_DEPTH", L))
    nlaunch = int(os.environ.get("MK_LAUNCHES", 1))
    dper = depth // nlaunch
    key = (dper, os.environ.get("MK_MODE", "full"))
    if key not in _NC_CACHE:
        _NC_CACHE[key] = build(dper)
    nc = _NC_CACHE[key]
    import time as _t
    _t0 = _t.time()
    xTs = [None] * NCORE
    parts = []
    for g in range(nlaunch):
        shared = _prep_shared(inputs, dper, g * dper)
        in_maps = [_core_inputs(inputs, shared, c, dper, g * dper, xTs[c]) for c in range(NCORE)]
        if os.environ.get("MK_VERBOSE"):
            print("[mk] host prep %.1fs" % (_t.time() - _t0), flush=True)
        res = run_bass_kernel_spmd(nc, in_maps, core_ids=list(range(NCORE)))
        if os.environ.get("MK_VERBOSE"):
            print("[mk] launch done %.1fs" % (_t.time() - _t0), flush=True)
        R = res.results
        xTs = [np.asarray(R[c]["yT"]) for c in range(NCORE)]
        parts.append(R)
        del shared, in_maps
    f = np.float32
    R = parts[-1]
    yp = np.empty((1, NCORE * TP, D), f)
    ys = np.empty((NCORE, TS, D), f)
    for c in range(NCORE):
        y = np.asarray(R[c]["yT"]).reshape(128, KC, T).transpose(2, 1, 0).reshape(T, D)
        yp[0, c * TP:(c + 1) * TP] = y[:TP]
        ys[c] = y[TP:]
    last = NCORE - 1
    npp = np.concatenate([np.asarray(Rg[last]["npp"]).reshape(dper, 128, 8, 15).transpose(0, 3, 2, 1).reshape(dper, 1, 15, 1024) for Rg in parts], axis=0)
    ngp = np.concatenate([np.asarray(Rg[last]["ngp"]).reshape(dper, 128, 4, 256).transpose(0, 2, 1, 3).reshape(dper, 1, 4, 128, 256) for Rg in parts], axis=0)
    nps = np.concatenate([np.stack([np.asarray(Rg[c]["nps"]).reshape(dper, 128, 8, 15).transpose(0, 3, 2, 1).reshape(dper, 15, 1024) for c in range(NCORE)], axis=1)
                          for Rg in parts], axis=0)
    ngs = np.concatenate([np.stack([np.asarray(Rg[c]["ngs"]).reshape(dper, 128, 4, 256).transpose(0, 2, 1, 3) for c in range(NCORE)], axis=1) for Rg in parts], axis=0)
    return (yp, ys, np.ascontiguousarray(npp), np.ascontiguousarray(ngp), np.ascontiguousarray(nps), np.ascontiguousarray(ngs))
```
